# Optimizing a Trainium2 kernel written in Bass

```python
import jax, jax.numpy as jnp
from jax import lax
import numpy as np

D_MODEL = 2048
BATCH = 2
SEQ = 8192
DEPTH = 1
DEC_BATCH = 1
DEC_SEQ = 8192
PAST_LEN = 128

N_HEADS = 16
Q_LORA = 512
KV_LORA = 512
QK_NOPE = 128
QK_ROPE = 64
QK_HEAD = QK_NOPE + QK_ROPE
V_HEAD = 128
ROPE_THETA = 10000.0
Q_BLOCK = 128
GMLP_GROUPS = 16
GMLP_GROUP_DIM = 128
GMLP_DIM = GMLP_GROUPS * GMLP_GROUP_DIM
CHUNK = 128
D_FF = 5632
N_BRANCH = 2
EPS = 1e-6
IN_SPLITS = (Q_LORA, KV_LORA, QK_ROPE, 2 * GMLP_DIM, N_BRANCH * D_MODEL)
D_IN = Q_LORA + KV_LORA + QK_ROPE + 2 * GMLP_DIM + N_BRANCH * D_MODEL

kernel_name = "mla_gmlp_macaron_encoder"


def rmsnorm(x, g):
    xf = x.astype(jnp.float32)
    y = xf * lax.rsqrt(jnp.mean(xf * xf, axis=-1, keepdims=True) + EPS)
    return (y * g.astype(jnp.float32)).astype(x.dtype)


def swiglu(x, w_gate, w_up, w_down):
    return (jax.nn.silu(x @ w_gate) * (x @ w_up)) @ w_down


def rope_tables(seq_len):
    inv = 1.0 / (ROPE_THETA ** (jnp.arange(0, QK_ROPE, 2, dtype=jnp.float32) / QK_ROPE))
    ang = jnp.arange(seq_len, dtype=jnp.float32)[:, None] * inv[None, :]
    return jnp.cos(ang)[:, None, :], jnp.sin(ang)[:, None, :]


def apply_rope(x, cos, sin):
    x1, x2 = jnp.split(x.astype(jnp.float32), 2, axis=-1)
    out = jnp.concatenate([x1 * cos - x2 * sin, x2 * cos + x1 * sin], axis=-1)
    return out.astype(x.dtype)


def mla_attention(q_c, kv_c, k_r, g_q, g_kv, w_uq, w_uk, w_uv, cos, sin):
    b, s, _ = q_c.shape
    q = (rmsnorm(q_c, g_q) @ w_uq).reshape(b, s, N_HEADS, QK_HEAD)
    q_nope = q[..., :QK_NOPE]
    q_rope = apply_rope(q[..., QK_NOPE:], cos, sin)
    c_kv = rmsnorm(kv_c, g_kv)
    k_nope = (c_kv @ w_uk).reshape(b, s, N_HEADS, QK_NOPE)
    v = (c_kv @ w_uv).reshape(b, s, N_HEADS, V_HEAD)
    k_rope = apply_rope(k_r[:, :, None, :], cos, sin)[:, :, 0]
    scale = QK_HEAD ** -0.5
    nb = s // Q_BLOCK
    qn_blk = q_nope.reshape(b, nb, Q_BLOCK, N_HEADS, QK_NOPE).transpose(1, 0, 2, 3, 4)
    qr_blk = q_rope.reshape(b, nb, Q_BLOCK, N_HEADS, QK_ROPE).transpose(1, 0, 2, 3, 4)

    def block(args):
        qn, qr = args
        logits = (jnp.einsum('bqhd,bkhd->bhqk', qn, k_nope)
                  + jnp.einsum('bqhr,bkr->bhqk', qr, k_rope)).astype(jnp.float32) * scale
        p = jax.nn.softmax(logits, axis=-1).astype(v.dtype)
        return jnp.einsum('bhqk,bkhd->bqhd', p, v)

    o = lax.map(block, (qn_blk, qr_blk))
    return o.transpose(1, 0, 2, 3, 4).reshape(b, s, N_HEADS * V_HEAD)


def spatial_gating(uv, g_v, w_s, b_s):
    b, s, _ = uv.shape
    z = jax.nn.gelu(uv)
    u, v = jnp.split(z, 2, axis=-1)
    v = rmsnorm(v, g_v).reshape(b, s // CHUNK, CHUNK, GMLP_GROUPS, GMLP_GROUP_DIM)
    v = jnp.einsum('gpq,bnqgc->bnpgc', w_s, v) + b_s.T[None, None, :, :, None]
    return u * v.reshape(b, s, GMLP_DIM)


def parallel_mixer(h, w_in, b_gate, g_q, w_uq, g_kv, w_uk, w_uv, w_o_attn,
                   g_v, w_s, b_s, w_o_gmlp, w_out, cos, sin):
    b, s, _ = h.shape
    proj = h @ w_in
    o1 = IN_SPLITS[0]
    o2 = o1 + IN_SPLITS[1]
    o3 = o2 + IN_SPLITS[2]
    o4 = o3 + IN_SPLITS[3]
    q_c, kv_c, k_r, uv, gate_pre = (proj[..., :o1], proj[..., o1:o2], proj[..., o2:o3],
                                     proj[..., o3:o4], proj[..., o4:])
    a = mla_attention(q_c, kv_c, k_r, g_q, g_kv, w_uq, w_uk, w_uv, cos, sin) @ w_o_attn
    c = spatial_gating(uv, g_v, w_s, b_s) @ w_o_gmlp
    gates = jax.nn.sigmoid(gate_pre + b_gate).reshape(b, s, N_BRANCH, D_MODEL)
    merged = gates[:, :, 0] * a + gates[:, :, 1] * c
    return merged @ w_out


def encoder(x, layer_weights, final_norm):
    cos, sin = rope_tables(x.shape[1])
    for l in range(DEPTH):
        (f1n, f1g, f1u, f1d, mn, w_in, b_gate, g_q, w_uq, g_kv, w_uk, w_uv, w_oa,
         g_v, w_s, b_s, w_og, w_out, f2n, f2g, f2u, f2d) = [w[l] for w in layer_weights]
        x = x + 0.5 * swiglu(rmsnorm(x, f1n), f1g, f1u, f1d)
        x = x + parallel_mixer(rmsnorm(x, mn), w_in, b_gate, g_q, w_uq, g_kv, w_uk, w_uv, w_oa,
                               g_v, w_s, b_s, w_og, w_out, cos, sin)
        x = x + 0.5 * swiglu(rmsnorm(x, f2n), f2g, f2u, f2d)
    return rmsnorm(x, final_norm)


def setup_inputs(seed: int = 0) -> dict:
    key = jax.random.key(seed)
    ks = jax.random.split(key, 32)

    def dense(k, shape, fan_in):
        return jax.random.normal(k, shape, jnp.float32) * (fan_in ** -0.5)

    def gain(k, shape):
        return 1.0 + 0.02 * jax.random.normal(k, shape, jnp.float32)

    L, D, F = DEPTH, D_MODEL, D_FF
    return {
        "x_prompt": jax.random.normal(ks[0], (BATCH, SEQ, D), jnp.float32),
        "x_sample": jax.random.normal(ks[1], (DEC_BATCH, DEC_SEQ, D), jnp.float32),
        "ffn1_norm": gain(ks[2], (L, D)),
        "ffn1_w_gate": dense(ks[3], (L, D, F), D),
        "ffn1_w_up": dense(ks[4], (L, D, F), D),
        "ffn1_w_down": dense(ks[5], (L, F, D), F),
        "mix_norm": gain(ks[6], (L, D)),
        "w_in": dense(ks[7], (L, D, D_IN), D),
        "b_gate": 0.02 * jax.random.normal(ks[8], (L, N_BRANCH * D), jnp.float32),
        "q_norm": gain(ks[9], (L, Q_LORA)),
        "w_uq": dense(ks[10], (L, Q_LORA, N_HEADS * QK_HEAD), Q_LORA),
        "kv_norm": gain(ks[11], (L, KV_LORA)),
        "w_uk": dense(ks[12], (L, KV_LORA, N_HEADS * QK_NOPE), KV_LORA),
        "w_uv": dense(ks[13], (L, KV_LORA, N_HEADS * V_HEAD), KV_LORA),
        "w_o_attn": dense(ks[14], (L, N_HEADS * V_HEAD, D), N_HEADS * V_HEAD),
        "gmlp_norm": gain(ks[15], (L, GMLP_DIM)),
        "w_s": dense(ks[16], (L, GMLP_GROUPS, CHUNK, CHUNK), CHUNK),
        "b_s": 1.0 + 0.02 * jax.random.normal(ks[17], (L, GMLP_GROUPS, CHUNK), jnp.float32),
        "w_o_gmlp": dense(ks[18], (L, GMLP_DIM, D), GMLP_DIM),
        "w_out": dense(ks[19], (L, D, D), D),
        "ffn2_norm": gain(ks[20], (L, D)),
        "ffn2_w_gate": dense(ks[21], (L, D, F), D),
        "ffn2_w_up": dense(ks[22], (L, D, F), D),
        "ffn2_w_down": dense(ks[23], (L, F, D), F),
        "final_norm": gain(ks[24], (D,)),
    }


def reference(x_prompt, x_sample, ffn1_norm, ffn1_w_gate, ffn1_w_up, ffn1_w_down,
              mix_norm, w_in, b_gate, q_norm, w_uq, kv_norm, w_uk, w_uv, w_o_attn,
              gmlp_norm, w_s, b_s, w_o_gmlp, w_out,
              ffn2_norm, ffn2_w_gate, ffn2_w_up, ffn2_w_down, final_norm):
    layer_weights = (ffn1_norm, ffn1_w_gate, ffn1_w_up, ffn1_w_down,
                     mix_norm, w_in, b_gate, q_norm, w_uq, kv_norm, w_uk, w_uv, w_o_attn,
                     gmlp_norm, w_s, b_s, w_o_gmlp, w_out,
                     ffn2_norm, ffn2_w_gate, ffn2_w_up, ffn2_w_down)
    y_prompt = encoder(x_prompt, layer_weights, final_norm)
    y_sample = encoder(x_sample, layer_weights, final_norm)
    return (y_prompt, y_sample)
```

```python
import contextlib
import numpy as np
import concourse.bass as bass
import concourse.mybir as mybir
from concourse.bass_utils import run_bass_kernel_spmd

F32 = mybir.dt.float32
BF16 = mybir.dt.bfloat16
AF = mybir.ActivationFunctionType
ALU = mybir.AluOpType
AX = mybir.AxisListType

NCORES = 8
D = 2048
KC = 16
FF = 5632
FC = 44
T = 512
NT = 6
SL = 1024
NSEQ = 3
SEQ = 8192
H = 16
LAT = 576
EPS = 1e-6
SCALE = 192 ** -0.5
GELU_K = 1.5957691216057308
WSLOT = 6144
NWS = 3

V_F1N, V_MIXN, V_F2N, V_FIN, V_GQ, V_GKV, V_GV, V_BGA, V_BGC = 0, 16, 32, 48, 64, 68, 72, 88, 104
NVEC = 120


class Tile:
    __slots__ = ("name", "w", "r", "pr", "sem", "cnt")

    def __init__(self, name):
        self.name = name
        self.w = []
        self.r = []
        self.pr = []
        self.sem = None
        self.cnt = 0


class Ins:
    __slots__ = ("eng", "fn", "deps", "kind", "tile", "val", "sig", "sigval")

    def __init__(self, eng, fn, kind):
        self.eng = eng
        self.fn = fn
        self.kind = kind
        self.deps = []
        self.tile = None
        self.val = 0
        self.sig = False
        self.sigval = 0


SIG_ROLL = 30000


class Sched:
    ENGS = ("pe", "act", "dve", "pool", "sp")

    def __init__(self, nc):
        self.nc = nc
        self.streams = {e: [] for e in self.ENGS}
        self.dma_tiles = []

    def _dep(self, ins, d, kind):
        if d is ins:
            return
        if d.kind == "c" and d.eng == ins.eng:
            if ins.eng == "pe" or kind == "war":
                return
        ins.deps.append(d)
        if d.kind == "c":
            d.sig = True

    def op(self, eng, fn, r=(), w=(), pw=(), kind="c", semtile=None):
        ins = Ins(eng, fn, kind)
        for t in r:
            for x in t.w:
                self._dep(ins, x, "raw")
        for t in w:
            for x in t.r:
                self._dep(ins, x, "war")
            for x in t.w:
                self._dep(ins, x, "waw")
        for t in pw:
            if t.r:
                for x in t.r:
                    self._dep(ins, x, "war")
                for x in t.w:
                    self._dep(ins, x, "waw")
            else:
                for x in t.pr:
                    self._dep(ins, x, "war")
        for t in r:
            t.r.append(ins)
        for t in w:
            t.w = [ins]
            t.pr = t.r
            t.r = []
        for t in pw:
            if t.r:
                t.w = [ins]
                t.pr = t.r
                t.r = []
            else:
                t.w.append(ins)
        if kind != "c":
            assert semtile is not None
            ins.tile = semtile
            if kind == "d":
                semtile.cnt += 16
            else:
                semtile.cnt += 1
            ins.val = semtile.cnt
            if semtile.sem is None:
                semtile.sem = True
                self.dma_tiles.append(semtile)
        self.streams[eng].append(ins)
        return ins

    def dma(self, q, out, in_, r=(), w=(), pw=(), semtile=None):
        if semtile is None:
            semtile = (list(w) + list(pw))[0]
        return self.op(q, lambda e, o=out, i=in_: e.dma_start(out=o, in_=i), r=r, w=w, pw=pw,
                       kind="d", semtile=semtile)

    def inherit(self, new_tiles, old_tiles):
        ws, rs = [], []
        seen = set()
        for t in old_tiles:
            for x in t.w + t.r + t.pr:
                if id(x) not in seen:
                    seen.add(id(x))
                    rs.append(x)
        for t in new_tiles:
            t.w = []
            t.r = list(rs)
            t.pr = list(rs)

    def emit(self, stack):
        nc = self.nc
        nsig = {}
        for e in self.ENGS:
            n = 0
            for ins in self.streams[e]:
                if ins.sig:
                    n += 1
                    ins.sigval = n
            nsig[e] = n
        esems = {}
        for e in self.ENGS:
            k = (nsig[e] + SIG_ROLL - 1) // SIG_ROLL
            esems[e] = [stack.enter_context(nc.semaphore("done_%s_%d" % (e, i))) for i in range(k)]
        for t in self.dma_tiles:
            t.sem = stack.enter_context(nc.semaphore("dma_" + t.name))
        self.nsem = sum(len(v) for v in esems.values()) + len(self.dma_tiles)

        def run(ename, eng):
            waited_e = {e: 0 for e in self.ENGS}
            waited_t = {}
            for ins in self.streams[ename]:
                need_e = {}
                need_t = {}
                for d in ins.deps:
                    if d.kind == "c":
                        if d.sigval > waited_e[d.eng] and d.sigval > need_e.get(d.eng, 0):
                            need_e[d.eng] = d.sigval
                    else:
                        k = id(d.tile)
                        if d.val > waited_t.get(k, 0) and d.val > need_t.get(k, (None, 0))[1]:
                            need_t[k] = (d.tile, d.val)
                for e, v in need_e.items():
                    si, sv = (v - 1) // SIG_ROLL, (v - 1) % SIG_ROLL + 1
                    eng.wait_ge(esems[e][si], sv)
                    waited_e[e] = v
                for k, (t, v) in need_t.items():
                    eng.wait_ge(t.sem, v)
                    waited_t[k] = v
                bi = ins.fn(eng)
                if ins.kind == "c":
                    if ins.sig:
                        v = ins.sigval
                        si = (v - 1) // SIG_ROLL
                        if (v - 1) % SIG_ROLL == 0 and si > 0:
                            pass
                        bi.then_inc(esems[ename][si], 1)
                elif ins.kind == "d":
                    bi.then_inc(ins.tile.sem, 16)
                else:
                    bi.then_inc(ins.tile.sem)

        block = stack.enter_context(nc.Block())

        @block.tensor
        def _(pe):
            run("pe", pe)

        @block.scalar
        def _(a):
            run("act", a)

        @block.vector
        def _(v):
            run("dve", v)

        @block.gpsimd
        def _(g):
            run("pool", g)

        @block.sync
        def _(sp):
            run("sp", sp)


class Buf:
    __slots__ = ("t", "tile")

    def __init__(self, t, tile):
        self.t = t
        self.tile = tile


class Rot:
    def __init__(self, items):
        self.items = items
        self.i = 0

    def next(self):
        x = self.items[self.i % len(self.items)]
        self.i += 1
        return x


def build_program(dbg=False):
    nc = bass.Bass("TRN2", target_bir_lowering=False)
    S = Sched(nc)

    def din(name, shape, dt=F32):
        return nc.dram_tensor(name, list(shape), dt, kind="ExternalInput").ap()

    x_d = din("x", [NSEQ * SL, D])
    w1g_d = din("w1g", [FC, 128, KC * 128])
    w1u_d = din("w1u", [FC, 128, KC * 128])
    w1d_d = din("w1d", [KC, 128, FC * 128])
    w2g_d = din("w2g", [FC, 128, KC * 128])
    w2u_d = din("w2u", [FC, 128, KC * 128])
    w2d_d = din("w2d", [KC, 128, FC * 128])
    wl_d = din("wl", [9, 128, KC * 128])
    wu_d = din("wu", [16, 128, KC * 128])
    wv_d = din("wv", [8, 128, 8 * 512])
    wmix_d = din("wmix", [16, 128, 3 * KC * 128])
    wog_d = din("wog", [16, 128, KC * 128])
    wo_d = din("wo", [16, 128, KC * 128])
    wh_d = din("wh", [H, 128, 4 * 512])
    wst_d = din("wst", [128, 2048])
    bsb_d = din("bsb", [128, 2048])
    vec_d = din("vec", [128, NVEC])
    rope_d = din("rope", [64, 2048])
    ident_d = din("ident", [128, 128])
    y_d = nc.dram_tensor("y", [NSEQ * SL, D], F32, kind="ExternalOutput").ap()

    dk = {"kind": "ExternalOutput"} if dbg else {}
    x1_s = nc.dram_tensor("x1_s", [NT, 128, KC * T], F32, **dk).ap()
    c_s = nc.dram_tensor("c_s", [NT, 128, KC * T], BF16, **dk).ap()
    lat_b = nc.dram_tensor("lat_b", [LAT, NSEQ * SL], BF16).ap()
    gath = nc.dram_tensor("gath", [NCORES * LAT, NSEQ * SL], BF16).ap()
    qn_s = nc.dram_tensor("qn_s", [NSEQ, 128, 4 * SL], BF16, **dk).ap()
    o_s = nc.dram_tensor("o_s", [NSEQ, 128, H * SL], BF16, **dk).ap()
    x1s_t = [Tile("x1s%d" % i) for i in range(NT)]
    cs_t = [Tile("cs%d" % i) for i in range(NT)]
    latb_t = Tile("latb")
    gath_t = Tile("gath")
    qns_t = [Tile("qns%d" % i) for i in range(NSEQ)]
    os_t = [Tile("os%d" % i) for i in range(NSEQ)]
    y_t = [Tile("y%d" % i) for i in range(NT)]

    off = [17408]

    def alloc(name, shape, dt, at=None):
        nbytes = int(np.prod(shape[1:])) * (4 if dt == F32 else 2)
        if at is None:
            at = off[0]
            off[0] += (nbytes + 63) // 64 * 64
        return nc.alloc_sbuf_tensor_at(name, list(shape), dt, offset=at)

    vec = Buf(alloc("vec", [128, NVEC], F32), Tile("vec"))
    ident = Buf(alloc("ident", [128, 128], F32), Tile("ident"))
    ones = Buf(alloc("ones", [128, 128], BF16), Tile("ones"))
    epsb = Buf(alloc("epsb", [128, 1], F32), Tile("epsb"))
    rope = Buf(alloc("rope", [64, 2048], F32), Tile("rope"))
    C0 = off[0]
    NPT = 6
    PT = [Buf(alloc("pt%d" % i, [128, 512], BF16), Tile("pt%d" % i)) for i in range(NPT)]
    whs = [Buf(alloc("whs%d" % i, [128, 4, 512], BF16), Tile("whs%d" % i)) for i in range(2)]
    qnl = Buf(alloc("qnl", [128, 4, SL], BF16), Tile("qnl"))
    rcp = [Buf(alloc("rcp%d" % i, [128, 512], F32), Tile("rcp%d" % i)) for i in range(2)]
    otl = [Buf(alloc("otl%d" % i, [128, 512], BF16), Tile("otl%d" % i)) for i in range(2)]
    A0 = off[0]
    xT = alloc("xT", [128, KC, T], F32)
    xT_t = [Tile("xT%d" % k) for k in range(KC)]
    hT = alloc("hT", [128, KC, T], BF16)
    hT_t = [Tile("hT%d" % k) for k in range(KC)]
    pool_off = off[0]
    pool = alloc("pool", [128, FC, T], BF16)
    pool_t = [Tile("pool%d" % k) for k in range(FC)]
    ws = [Buf(alloc("ws%d" % i, [128, WSLOT], BF16), Tile("ws%d" % i)) for i in range(NWS)]
    ftmp = [Buf(alloc("ftmp%d" % i, [128, 512], F32), Tile("ftmp%d" % i)) for i in range(4)]
    qf = [Buf(alloc("qf%d" % i, [128, 512], F32), Tile("qf%d" % i)) for i in range(4)]
    btmp = [Buf(alloc("btmp%d" % i, [128, 512], BF16), Tile("btmp%d" % i)) for i in range(4)]
    rstd = [Buf(alloc("rstd%d" % i, [128, 512], F32), Tile("rstd%d" % i)) for i in range(2)]
    latq = Buf(alloc("latq", [128, 4, T], BF16), Tile("latq"))
    latkv = Buf(alloc("latkv", [128, 4, T], BF16), Tile("latkv"))
    latkr = Buf(alloc("latkr", [64, T], BF16), Tile("latkr"))
    ss = Buf(alloc("ss", [128, 16], F32), Tile("ss"))
    rs4 = Buf(alloc("rs4", [128, 4], F32), Tile("rs4"))
    bsb = Buf(alloc("bsb", [128, 2048], F32, at=C0 + 2048), Tile("bsb"))
    wst = Buf(alloc("wst", [128, 2048], BF16, at=C0 + 2048 + 8192), Tile("wst"))
    A1 = off[0]
    stg = [alloc("stg%d" % k, [128, 2048], F32, at=pool_off + k * 8192) for k in range(2)]
    stg_t = [pool_t[0:8], pool_t[8:16]]
    o2 = [A0]

    def alloc2(name, shape, dt):
        nbytes = int(np.prod(shape[1:])) * (4 if dt == F32 else 2)
        at = o2[0]
        o2[0] += (nbytes + 63) // 64 * 64
        assert o2[0] <= A1, (name, o2[0], A1)
        return nc.alloc_sbuf_tensor_at(name, list(shape), dt, offset=at)

    CK = Buf(alloc2("CK", [128, 4, SEQ], BF16), Tile("CK"))
    KR = Buf(alloc2("KR", [64, SEQ], BF16), Tile("KR"))
    KT = [Buf(alloc2("KT%d" % i, [128, SEQ], BF16), Tile("KT%d" % i)) for i in range(2)]
    VV = [Buf(alloc2("VV%d" % i, [128, 64, 128], BF16), Tile("VV%d" % i)) for i in range(2)]
    QN = [Buf(alloc2("QN%d" % i, [128, SL], BF16), Tile("QN%d" % i)) for i in range(2)]
    QR = [Buf(alloc2("QR%d" % i, [64, SL], BF16), Tile("QR%d" % i)) for i in range(2)]
    accD = Buf(alloc2("accD", [128, 512], F32), Tile("accD"))
    accP = Buf(alloc2("accP", [128, 512], F32), Tile("accP"))
    ones32 = Buf(alloc2("ones32", [128, 128], F32), Tile("ones32"))
    assert off[0] <= 229376, off[0]

    regionA_tiles = (xT_t + hT_t + pool_t + [b.tile for b in ws + ftmp + qf + btmp + rstd]
                     + [latq.tile, latkv.tile, latkr.tile, ss.tile, rs4.tile, bsb.tile, wst.tile])
    phase2_tiles = [CK.tile, KR.tile, accD.tile, accP.tile, ones32.tile] + [b.tile for b in KT + VV + QN + QR + PT + whs]

    psum = nc.alloc_psum_tensor("psum", [128, 8, 512], F32)
    ps_t = [Tile("ps%d" % k) for k in range(8)]

    class PS:
        __slots__ = ("ap", "tile", "idx")

        def __init__(self, k):
            self.idx = k
            self.ap = psum[:, k, :]
            self.tile = ps_t[k]

    PSB = [PS(k) for k in range(8)]
    psrot = Rot(PSB)
    wsrot = Rot(ws)
    ftrot = Rot(ftmp)
    btrot = Rot(btmp)
    rsrot = Rot(rstd)

    def vcol(c):
        return vec.t[:, c:c + 1]

    evac_flip = [0]

    def wload(dram, g0, G, n, q="pool"):
        slot = wsrot.next()
        dst = slot.t[:, 0:G * n].rearrange("p (g n) -> p g n", g=G)
        src = dram[g0:g0 + G].rearrange("g p n -> p g n")
        S.dma(q, dst, src, w=[slot.tile])
        return slot, dst

    def mm_group(ps, pairs, r_tiles, split=None):
        n = len(pairs)

        def fn(pe, pairs=pairs, ps=ps):
            bi = None
            for k, (l, rr) in enumerate(pairs):
                bi = pe.matmul(ps.ap, lhsT=l, rhs=rr, start=(k == 0), stop=(k == n - 1))
            return bi
        S.op("pe", fn, r=r_tiles, w=[ps.tile])

    def rmsnorm_fm(src_aps, src_tiles, n, gcol0, dst_aps, dst_tiles, dn):
        psn = psrot.next()
        for k in range(n):
            sq = btrot.next()
            S.op("act", lambda a, o=sq.t[:], i=src_aps[k]: a.activation(out=o, in_=i, func=AF.Square),
                 r=[src_tiles[k]], w=[sq.tile])
            S.op("pe", lambda pe, o=psn.ap, rr=sq.t[:], k=k: pe.matmul(o, lhsT=ones.t[:], rhs=rr,
                                                                       start=(k == 0), stop=(k == n - 1)),
                 r=[sq.tile, ones.tile], **({"w": [psn.tile]} if k == 0 else {"pw": [psn.tile]}))
        rs = rsrot.next()
        S.op("act", lambda a, o=rs.t[:], i=psn.ap: a.activation(out=o, in_=i, func=AF.Sqrt, scale=1.0 / dn,
                                                                  bias=epsb.t[:]),
             r=[psn.tile, epsb.tile], w=[rs.tile])
        S.op("dve", lambda v, o=rs.t[:]: v.reciprocal(out=o, in_=o), r=[rs.tile], w=[rs.tile])
        for k in range(n):
            S.op("dve", lambda v, o=dst_aps[k], i=src_aps[k], g=vcol(gcol0 + k), rr=rs.t[:]:
                 v.scalar_tensor_tensor(o, i, g, rr, ALU.mult, ALU.mult),
                 r=[src_tiles[k], rs.tile, vec.tile], w=[dst_tiles[k]])

    xT_aps = [xT[:, k, :] for k in range(KC)]
    hT_aps = [hT[:, k, :] for k in range(KC)]
    pool_aps = [pool[:, k, :] for k in range(FC)]

    def norm_x_to_h(gcol0):
        rmsnorm_fm(xT_aps, xT_t, KC, gcol0, hT_aps, hT_t, D)

    def ffn(wg_d, wu_d, wd_d):
        for jg in range(FC // 2):
            sg, sgv = wload(wg_d, jg * 2, 2, KC * 128)
            su, suv = wload(wu_d, jg * 2, 2, KC * 128)
            for jj in range(2):
                j = jg * 2 + jj
                pg = psrot.next()
                mm_group(pg, [(sg.t[:, (jj * KC + k) * 128:(jj * KC + k + 1) * 128], hT_aps[k]) for k in range(KC)],
                         [sg.tile] + hT_t)
                pu = psrot.next()
                mm_group(pu, [(su.t[:, (jj * KC + k) * 128:(jj * KC + k + 1) * 128], hT_aps[k]) for k in range(KC)],
                         [su.tile] + hT_t)
                tmp = ftrot.next()
                S.op("act", lambda a, o=tmp.t[:], i=pg.ap: a.activation(out=o, in_=i, func=AF.Silu),
                     r=[pg.tile], w=[tmp.tile])
                S.op("dve", lambda v, o=pool_aps[j], a_=tmp.t[:], b_=pu.ap: v.tensor_tensor(o, a_, b_, ALU.mult),
                     r=[tmp.tile, pu.tile], w=[pool_t[j]])
        for m in range(KC):
            sd, sdv = wload(wd_d, m, 1, FC * 128)
            pd = psrot.next()
            mm_group(pd, [(sd.t[:, j * 128:(j + 1) * 128], pool_aps[j]) for j in range(FC)],
                     [sd.tile] + pool_t)
            S.op("dve", lambda v, o=xT_aps[m], p=pd.ap: v.scalar_tensor_tensor(o, p, 0.5, o, ALU.mult, ALU.add),
                 r=[pd.tile, xT_t[m]], w=[xT_t[m]])

    def gelu(ps, dst_ap, dst_tiles, pw=False):
        tmp = ftrot.next()
        S.op("act", lambda a, o=tmp.t[:], i=ps.ap: a.activation(out=o, in_=i, func=AF.Square),
             r=[ps.tile], w=[tmp.tile])
        S.op("dve", lambda v, o=tmp.t[:]: v.tensor_scalar(o, o, 0.044715, 1.0, ALU.mult, ALU.add),
             r=[tmp.tile], w=[tmp.tile])
        S.op("dve", lambda v, o=tmp.t[:], p=ps.ap: v.tensor_tensor(o, o, p, ALU.mult),
             r=[tmp.tile, ps.tile], w=[tmp.tile])
        S.op("act", lambda a, o=tmp.t[:]: a.activation(out=o, in_=o, func=AF.Sigmoid, scale=GELU_K),
             r=[tmp.tile], w=[tmp.tile])
        kw = {"pw": dst_tiles} if pw else {"w": dst_tiles}
        S.op("dve", lambda v, o=dst_ap, a_=tmp.t[:], p=ps.ap: v.tensor_tensor(o, a_, p, ALU.mult),
             r=[tmp.tile, ps.tile], **kw)

    def copy_evac(ps_ap, ps_tile, dst_ap, dst_tiles, pw=False, eng=None):
        if eng is None:
            eng = "act" if (evac_flip[0] % 2 == 0) else "dve"
            evac_flip[0] += 1
        kw = {"pw": dst_tiles} if pw else {"w": dst_tiles}
        if eng == "act":
            S.op("act", lambda a, o=dst_ap, i=ps_ap: a.copy(out=o, in_=i), r=[ps_tile], **kw)
        else:
            S.op("dve", lambda v, o=dst_ap, i=ps_ap: v.tensor_copy(out=o, in_=i), r=[ps_tile], **kw)

    S.dma("sp", vec.t[:], vec_d, w=[vec.tile])
    S.dma("sp", ident.t[:], ident_d, w=[ident.tile])
    S.dma("sp", rope.t[:], rope_d, w=[rope.tile])
    S.dma("sp", bsb.t[:], bsb_d, w=[bsb.tile])
    S.dma("pool", wst.t[:], wst_d, w=[wst.tile])
    S.op("dve", lambda v: v.memset(ones.t[:], 1.0), w=[ones.tile])
    S.op("dve", lambda v: v.memset(epsb.t[:], EPS), w=[epsb.tile])

    cosT = rope.t[:, 0:SL]
    sinT = rope.t[:, SL:2 * SL]

    def load_transpose_x(i):
        for r in range(4):
            k = r % 2
            S.dma("sp", stg[k][:], x_d[i * T + r * 128:i * T + (r + 1) * 128, :], w=stg_t[k])
            for q4 in range(4):
                ps = psrot.next()

                def fn(pe, ps=ps, k=k, q4=q4):
                    bi = None
                    for kk in range(4):
                        c = q4 * 4 + kk
                        bi = pe.transpose(out=ps.ap[:, kk * 128:(kk + 1) * 128], in_=stg[k][:, c * 128:(c + 1) * 128],
                                          identity=ident.t[:])
                    return bi
                S.op("pe", fn, r=stg_t[k] + [ident.tile], w=[ps.tile])
                copy_evac(ps.ap.rearrange("p (a b) -> p a b", a=4), ps.tile,
                          xT[:, q4 * 4:(q4 + 1) * 4, r * 128:(r + 1) * 128], xT_t[q4 * 4:(q4 + 1) * 4], pw=True)

    def phase1_tile(i):
        s, half = i // 2, i % 2
        tcol = s * SL + half * T
        load_transpose_x(i)
        norm_x_to_h(V_F1N)
        ffn(w1g_d, w1u_d, w1d_d)
        S.dma("sp", x1_s[i], xT[:, :, :].rearrange("p a b -> p (a b)"), r=xT_t, w=[x1s_t[i]])
        norm_x_to_h(V_MIXN)
        for g0 in range(0, 9, 3):
            sl, slv = wload(wl_d, g0, 3, KC * 128)
            for gi in range(3):
                c = g0 + gi
                if c < 8:
                    ps = psrot.next()
                    mm_group(ps, [(sl.t[:, (gi * KC + k) * 128:(gi * KC + k + 1) * 128], hT_aps[k]) for k in range(KC)],
                             [sl.tile] + hT_t)
                    q = qf[c % 4]
                    copy_evac(ps.ap, ps.tile, q.t[:], [q.tile], eng="act")
                    if c % 4 == 3:
                        dstb = latq if c == 3 else latkv
                        rmsnorm_fm([qf[k].t[:] for k in range(4)], [qf[k].tile for k in range(4)], 4,
                                   V_GQ if c == 3 else V_GKV,
                                   [dstb.t[:, k, :] for k in range(4)], [dstb.tile] * 4, 512)
                        if c == 3:
                            S.dma("sp", qn_s[s].rearrange("p (c t) -> p c t", c=4)[:, :, half * T:(half + 1) * T],
                                  latq.t[:], r=[latq.tile], pw=[qns_t[s]])
                        else:
                            S.dma("sp", lat_b[0:512, tcol:tcol + T].rearrange("(c p) t -> p c t", p=128),
                                  latkv.t[:], r=[latkv.tile], pw=[latb_t])
                else:
                    pk = psrot.next()
                    pks = psrot.next()
                    for (pp, c0) in ((pk, 0), (pks, 64)):
                        def fn(pe, pp=pp, c0=c0, sl=sl, gi=gi):
                            bi = None
                            for k in range(KC):
                                base = (gi * KC + k) * 128 + c0
                                bi = pe.matmul(pp.ap[0:64, :], lhsT=sl.t[:, base:base + 64], rhs=hT_aps[k],
                                               start=(k == 0), stop=(k == KC - 1))
                            return bi
                        S.op("pe", fn, r=[sl.tile] + hT_t, w=[pp.tile])
                    t1 = ftrot.next()
                    t2 = ftrot.next()
                    S.op("dve", lambda v, o=t1.t[0:64, :], p=pk.ap[0:64, :], c_=cosT[:, half * T:(half + 1) * T]:
                         v.tensor_tensor(o, p, c_, ALU.mult), r=[pk.tile, rope.tile], w=[t1.tile])
                    S.op("dve", lambda v, o=t2.t[0:64, :], p=pks.ap[0:64, :], c_=sinT[:, half * T:(half + 1) * T]:
                         v.tensor_tensor(o, p, c_, ALU.mult), r=[pks.tile, rope.tile], w=[t2.tile])
                    S.op("dve", lambda v, o=latkr.t[:], a_=t1.t[0:64, :], b_=t2.t[0:64, :]:
                         v.tensor_tensor(o, a_, b_, ALU.add), r=[t1.tile, t2.tile], w=[latkr.tile])
                    S.dma("sp", lat_b[512:576, tcol:tcol + T], latkr.t[:], r=[latkr.tile], pw=[latb_t])
        for g0 in range(0, 16, 3):
            G = min(3, 16 - g0)
            su, suv = wload(wu_d, g0, G, KC * 128)
            for gi in range(G):
                c = g0 + gi
                ps = psrot.next()
                mm_group(ps, [(su.t[:, (gi * KC + k) * 128:(gi * KC + k + 1) * 128], hT_aps[k]) for k in range(KC)],
                         [su.tile] + hT_t)
                gelu(ps, pool_aps[c], [pool_t[c]])
        vtok = [pool[:, 16 + 4 * r:20 + 4 * r, :].rearrange("p a b -> p (a b)") for r in range(4)]
        vtok_t = [pool_t[16 + 4 * r:20 + 4 * r] for r in range(4)]
        pv = PSB[0:4]
        S.op("dve", lambda v: v.memset(ss.t[:], 0.0), w=[ss.tile])
        for f in range(4):
            for kh in range(2):
                sv, svv = wload(wv_d, f * 2 + kh, 1, 8 * 512)
                for r in range(4):
                    def fn(pe, r=r, kh=kh, sv=sv):
                        bi = None
                        for kk in range(8):
                            k = kh * 8 + kk
                            bi = pe.matmul(pv[r].ap, lhsT=hT[:, k, r * 128:(r + 1) * 128],
                                           rhs=sv.t[:, kk * 512:(kk + 1) * 512],
                                           start=(k == 0), stop=(k == KC - 1))
                        return bi
                    S.op("pe", fn, r=[sv.tile] + hT_t[kh * 8:(kh + 1) * 8],
                         **({"w": [pv[r].tile]} if kh == 0 else {"pw": [pv[r].tile]}))
            for r in range(4):
                gelu(pv[r], vtok[r][:, f * 512:(f + 1) * 512], vtok_t[r][f:f + 1])
                junk = btrot.next()
                S.op("act", lambda a, o=junk.t[:], i=vtok[r][:, f * 512:(f + 1) * 512],
                     acc=ss.t[:, r * 4 + f:r * 4 + f + 1]: a.activation(out=o, in_=i, func=AF.Square, accum_out=acc),
                     r=vtok_t[r][f:f + 1], w=[junk.tile], pw=[ss.tile])
        S.op("dve", lambda v: v.tensor_reduce(out=rs4.t[:], in_=ss.t[:].rearrange("p (r f) -> p r f", r=4),
                                              axis=AX.X, op=ALU.add), r=[ss.tile], w=[rs4.tile])
        S.op("act", lambda a: a.activation(out=rs4.t[:], in_=rs4.t[:], func=AF.Sqrt, scale=1.0 / D, bias=epsb.t[:]),
             r=[rs4.tile, epsb.tile], w=[rs4.tile])
        S.op("dve", lambda v: v.reciprocal(out=rs4.t[:], in_=rs4.t[:]), r=[rs4.tile], w=[rs4.tile])
        for r in range(4):
            S.op("dve", lambda v, o=vtok[r], sc=rs4.t[:, r:r + 1]: v.tensor_scalar(o, o, sc, None, ALU.mult),
                 r=vtok_t[r] + [rs4.tile], w=vtok_t[r])
        for g in range(16):
            ps = psrot.next()

            def fn(pe, ps=ps, g=g):
                bi = None
                for r in range(4):
                    bi = pe.matmul(ps.ap[:, r * 128:(r + 1) * 128], lhsT=vtok[r][:, g * 128:(g + 1) * 128],
                                   rhs=wst.t[:, g * 128:(g + 1) * 128], start=True, stop=True)
                return bi
            S.op("pe", fn, r=[pool_t[16 + 4 * r + g // 4] for r in range(4)] + [wst.tile], w=[ps.tile])
            tmp = ftrot.next()
            for r in range(4):
                S.op("dve", lambda v, o=tmp.t[:, r * 128:(r + 1) * 128], p=ps.ap[:, r * 128:(r + 1) * 128],
                     gv=vcol(V_GV + g), b_=bsb.t[:, g * 128:(g + 1) * 128]:
                     v.scalar_tensor_tensor(o, p, gv, b_, ALU.mult, ALU.add),
                     r=[ps.tile, vec.tile, bsb.tile], **({"w": [tmp.tile]} if r == 0 else {"pw": [tmp.tile]}))
            S.op("dve", lambda v, o=pool_aps[g], a_=tmp.t[:]: v.tensor_tensor(o, a_, o, ALU.mult),
                 r=[tmp.tile, pool_t[g]], w=[pool_t[g]])
        for g0 in range(0, 16, 3):
            G = min(3, 16 - g0)
            sw, swv = wload(wog_d, g0, G, KC * 128)
            for gi in range(G):
                m = g0 + gi
                ps = psrot.next()
                mm_group(ps, [(sw.t[:, (gi * KC + k) * 128:(gi * KC + k + 1) * 128], pool_aps[k]) for k in range(KC)],
                         [sw.tile] + pool_t[0:16])
                copy_evac(ps.ap, ps.tile, pool_aps[16 + m], [pool_t[16 + m]])
        S.dma("sp", c_s[i], pool[:, 16:32, :].rearrange("p a b -> p (a b)"), r=pool_t[16:32], w=[cs_t[i]])

    for i in range(NT):
        phase1_tile(i)

    S.op("pool", lambda g: g.collective_compute("AllGather", ALU.bypass,
                                                replica_groups=[list(range(NCORES))],
                                                ins=[lat_b.opt()], outs=[gath.opt()]),
         r=[latb_t], w=[gath_t], kind="cc", semtile=gath_t)

    S.inherit(phase2_tiles, regionA_tiles)
    pS = PSB[0:3]
    pO = PSB[3]
    pD = PSB[4]
    pgen = Rot(PSB[5:8])
    S.op("dve", lambda v: v.memset(ones32.t[:], 1.0), w=[ones32.tile])

    def gen_head(s, h):
        k = h % 2
        w = whs[k]
        S.dma("pool", w.t[:], wh_d[h].rearrange("p (k n) -> p k n", k=4), w=[w.tile])
        for qb in range(2):
            ps = pgen.next()
            mm_group(ps, [(w.t[:, kc, 0:128], qnl.t[:, kc, qb * 512:(qb + 1) * 512]) for kc in range(4)],
                     [w.tile, qnl.tile])
            copy_evac(ps.ap, ps.tile, QN[k].t[:, qb * 512:(qb + 1) * 512], [QN[k].tile], pw=True, eng="dve")
            p1 = pgen.next()
            p2 = pgen.next()
            for (pp, c0) in ((p1, 128), (p2, 192)):
                def fn(pe, pp=pp, c0=c0, w=w, qb=qb):
                    bi = None
                    for kc in range(4):
                        bi = pe.matmul(pp.ap[0:64, :], lhsT=w.t[:, kc, c0:c0 + 64],
                                       rhs=qnl.t[:, kc, qb * 512:(qb + 1) * 512], start=(kc == 0), stop=(kc == 3))
                    return bi
                S.op("pe", fn, r=[w.tile, qnl.tile], w=[pp.tile])
            t1 = rcp[0]
            t2 = rcp[1]
            S.op("dve", lambda v, o=t1.t[0:64, :], p=p1.ap[0:64, :], c_=cosT[:, qb * 512:(qb + 1) * 512]:
                 v.tensor_tensor(o, p, c_, ALU.mult), r=[p1.tile, rope.tile], w=[t1.tile])
            S.op("dve", lambda v, o=t2.t[0:64, :], p=p2.ap[0:64, :], c_=sinT[:, qb * 512:(qb + 1) * 512]:
                 v.tensor_tensor(o, p, c_, ALU.mult), r=[p2.tile, rope.tile], w=[t2.tile])
            S.op("dve", lambda v, o=QR[k].t[:, qb * 512:(qb + 1) * 512], a_=t1.t[0:64, :], b_=t2.t[0:64, :]:
                 v.tensor_tensor(o, a_, b_, ALU.add), r=[t1.tile, t2.tile], pw=[QR[k].tile])
        for tb in range(16):
            ps = pgen.next()
            mm_group(ps, [(w.t[:, kc, 256:384], CK.t[:, kc, tb * 512:(tb + 1) * 512]) for kc in range(4)],
                     [w.tile, CK.tile])
            copy_evac(ps.ap, ps.tile, KT[k].t[:, tb * 512:(tb + 1) * 512], [KT[k].tile], pw=True, eng="dve")
        for t4 in range(16):
            ps = pgen.next()

            def fn(pe, ps=ps, t4=t4, w=w):
                bi = None
                for j in range(4):
                    tt = t4 * 4 + j
                    for kc in range(4):
                        bi = pe.matmul(ps.ap[:, j * 128:(j + 1) * 128], lhsT=CK.t[:, kc, tt * 128:(tt + 1) * 128],
                                       rhs=w.t[:, kc, 384:512], start=(kc == 0), stop=(kc == 3))
                return bi
            S.op("pe", fn, r=[w.tile, CK.tile], w=[ps.tile])
            copy_evac(ps.ap, ps.tile, VV[k].t[:, t4 * 4:(t4 + 1) * 4, :].rearrange("p a b -> p (a b)"),
                      [VV[k].tile], pw=True, eng="dve")

    def attn_head(s, h):
        k = h % 2
        for qb in range(2):
            qsl = slice(qb * 512, (qb + 1) * 512)
            NKT = 64
            pts = {}

            def qk(kt):
                ps = pS[kt % 3]

                def fn(pe, ps=ps, kt=kt, qsl=qsl):
                    pe.matmul(ps.ap, lhsT=KT[k].t[:, kt * 128:(kt + 1) * 128], rhs=QN[k].t[:, qsl],
                              start=True, stop=False)
                    return pe.matmul(ps.ap, lhsT=KR.t[:, kt * 128:(kt + 1) * 128], rhs=QR[k].t[:, qsl],
                                     start=False, stop=True)
                S.op("pe", fn, r=[KT[k].tile, QN[k].tile, KR.tile, QR[k].tile], w=[ps.tile])

            def ex(kt):
                ps = pS[kt % 3]
                pt = PT[kt % NPT]
                S.op("act", lambda a, o=pt.t[:], i=ps.ap: a.activation(out=o, in_=i, func=AF.Exp, scale=SCALE),
                     r=[ps.tile], w=[pt.tile])
                if kt % 3 != 2:
                    eng, acc, first = "dve", accD, (kt == 0)
                else:
                    eng, acc, first = "pool", accP, (kt == 2)
                if first:
                    S.op(eng, lambda e, o=acc.t[:], i=pt.t[:]: e.tensor_copy(out=o, in_=i), r=[pt.tile], w=[acc.tile])
                else:
                    S.op(eng, lambda e, o=acc.t[:], i=pt.t[:]: e.tensor_tensor(o, o, i, ALU.add),
                         r=[pt.tile, acc.tile], w=[acc.tile])

            def pvm(kt):
                pt = PT[kt % NPT]

                def fn(pe, pt=pt, kt=kt):
                    return pe.matmul(pO.ap, lhsT=VV[k].t[:, kt, :], rhs=pt.t[:], start=(kt == 0), stop=(kt == NKT - 1))
                kw = {"w": [pO.tile]} if kt == 0 else {"pw": [pO.tile]}
                S.op("pe", fn, r=[VV[k].tile, pt.tile], **kw)

            qk(0)
            qk(1)
            for kt in range(NKT):
                ex(kt)
                if kt + 2 < NKT:
                    qk(kt + 2)
                pvm(kt)
            rc = rcp[qb]
            ot = otl[qb]

            def fnd(pe):
                pe.matmul(pD.ap, lhsT=ones32.t[:], rhs=accD.t[:], start=True, stop=False)
                return pe.matmul(pD.ap, lhsT=ones32.t[:], rhs=accP.t[:], start=False, stop=True)
            S.op("pe", fnd, r=[ones32.tile, accD.tile, accP.tile], w=[pD.tile])
            S.op("dve", lambda v, o=rc.t[:], i=pD.ap: v.reciprocal(out=o, in_=i), r=[pD.tile], w=[rc.tile])
            S.op("dve", lambda v, o=ot.t[:], a_=rc.t[:], p=pO.ap: v.tensor_tensor(o, a_, p, ALU.mult),
                 r=[rc.tile, pO.tile], w=[ot.tile])
            S.dma("sp", o_s[s].rearrange("p (h t) -> p h t", h=H)[:, h, qsl], ot.t[:], r=[ot.tile], pw=[os_t[s]])

    for s in range(NSEQ):
        for rk in range(NCORES):
            S.dma("sp", CK.t[:, :, rk * SL:(rk + 1) * SL],
                  gath[rk * LAT:rk * LAT + 512, s * SL:(s + 1) * SL].rearrange("(c p) t -> p c t", p=128),
                  r=[gath_t], **({"w": [CK.tile]} if rk == 0 else {"pw": [CK.tile]}))
            S.dma("sp", KR.t[:, rk * SL:(rk + 1) * SL],
                  gath[rk * LAT + 512:rk * LAT + 576, s * SL:(s + 1) * SL],
                  r=[gath_t], **({"w": [KR.tile]} if rk == 0 else {"pw": [KR.tile]}))
        S.dma("sp", qnl.t[:], qn_s[s].rearrange("p (c t) -> p c t", c=4), r=[qns_t[s]], w=[qnl.tile])
        gen_head(s, 0)
        for h in range(H):
            if h + 1 < H:
                gen_head(s, h + 1)
            attn_head(s, h)

    S.inherit(regionA_tiles, phase2_tiles)
    psrot.i = 0

    def phase3_tile(i):
        s, half = i // 2, i % 2
        S.dma("sp", xT[:, :, :].rearrange("p a b -> p (a b)"), x1_s[i], r=[x1s_t[i]], w=xT_t)
        S.dma("sp", pool[:, 0:16, :], o_s[s].rearrange("p (h t) -> p h t", h=H)[:, :, half * T:(half + 1) * T],
              r=[os_t[s]], w=pool_t[0:16])
        S.dma("sp", pool[:, 16:32, :].rearrange("p a b -> p (a b)"), c_s[i], r=[cs_t[i]], w=pool_t[16:32])
        norm_x_to_h(V_MIXN)
        for m in range(KC):
            sw, swv = wload(wmix_d, m, 1, 3 * KC * 128)
            pga = psrot.next()
            mm_group(pga, [(sw.t[:, k * 128:(k + 1) * 128], hT_aps[k]) for k in range(KC)], [sw.tile] + hT_t)
            pgc = psrot.next()
            mm_group(pgc, [(sw.t[:, (KC + k) * 128:(KC + k + 1) * 128], hT_aps[k]) for k in range(KC)],
                     [sw.tile] + hT_t)
            pa = psrot.next()
            mm_group(pa, [(sw.t[:, (2 * KC + k) * 128:(2 * KC + k + 1) * 128], pool_aps[k]) for k in range(KC)],
                     [sw.tile] + pool_t[0:16])
            ga = ftrot.next()
            gc = ftrot.next()
            S.op("act", lambda a, o=ga.t[:], i=pga.ap, b_=vcol(V_BGA + m): a.activation(out=o, in_=i, func=AF.Sigmoid,
                                                                                       bias=b_),
                 r=[pga.tile, vec.tile], w=[ga.tile])
            S.op("act", lambda a, o=gc.t[:], i=pgc.ap, b_=vcol(V_BGC + m): a.activation(out=o, in_=i, func=AF.Sigmoid,
                                                                                       bias=b_),
                 r=[pgc.tile, vec.tile], w=[gc.tile])
            S.op("dve", lambda v, o=ga.t[:], p=pa.ap: v.tensor_tensor(o, o, p, ALU.mult),
                 r=[ga.tile, pa.tile], w=[ga.tile])
            S.op("dve", lambda v, o=gc.t[:], c_=pool_aps[16 + m]: v.tensor_tensor(o, o, c_, ALU.mult),
                 r=[gc.tile, pool_t[16 + m]], w=[gc.tile])
            S.op("dve", lambda v, o=pool_aps[16 + m], a_=ga.t[:], b_=gc.t[:]: v.tensor_tensor(o, a_, b_, ALU.add),
                 r=[ga.tile, gc.tile], w=[pool_t[16 + m]])
        for g0 in range(0, 16, 3):
            G = min(3, 16 - g0)
            sw, swv = wload(wo_d, g0, G, KC * 128)
            for gi in range(G):
                m = g0 + gi
                ps = psrot.next()
                mm_group(ps, [(sw.t[:, (gi * KC + k) * 128:(gi * KC + k + 1) * 128], pool_aps[16 + k])
                              for k in range(KC)], [sw.tile] + pool_t[16:32])
                S.op("dve", lambda v, o=xT_aps[m], p=ps.ap: v.tensor_tensor(o, o, p, ALU.add),
                     r=[ps.tile, xT_t[m]], w=[xT_t[m]])
        norm_x_to_h(V_F2N)
        ffn(w2g_d, w2u_d, w2d_d)
        rmsnorm_fm(xT_aps, xT_t, KC, V_FIN, xT_aps, xT_t, D)
        for r in range(4):
            k = r % 2
            for q4 in range(4):
                ps = psrot.next()

                def fn(pe, ps=ps, q4=q4, r=r):
                    bi = None
                    for kk in range(4):
                        c = q4 * 4 + kk
                        bi = pe.transpose(out=ps.ap[:, kk * 128:(kk + 1) * 128], in_=xT[:, c, r * 128:(r + 1) * 128],
                                          identity=ident.t[:])
                    return bi
                S.op("pe", fn, r=xT_t[q4 * 4:(q4 + 1) * 4] + [ident.tile], w=[ps.tile])
                copy_evac(ps.ap, ps.tile, stg[k][:, q4 * 512:(q4 + 1) * 512], stg_t[k][q4 * 2:q4 * 2 + 2])
            S.dma("sp", y_d[i * T + r * 128:i * T + (r + 1) * 128, :], stg[k][:], r=stg_t[k], pw=[y_t[i]])

    for i in range(NT):
        phase3_tile(i)

    S.op("sp", lambda e: None, r=y_t)

    stack = contextlib.ExitStack()
    with stack:
        S.emit(stack)
    return nc


def _lay_fm(W, ncol=None):
    K, N = W.shape
    kc, nm = K // 128, N // 128
    return np.ascontiguousarray(W.reshape(kc, 128, nm, 128).transpose(2, 1, 0, 3).reshape(nm, 128, kc * 128))


_PROGRAM = None


def _prep_weights(ffn1_norm, ffn1_w_gate, ffn1_w_up, ffn1_w_down, mix_norm, w_in, b_gate, q_norm, w_uq,
                  kv_norm, w_uk, w_uv, w_o_attn, gmlp_norm, w_s, b_s, w_o_gmlp, w_out,
                  ffn2_norm, ffn2_w_gate, ffn2_w_up, ffn2_w_down, final_norm):
    f = lambda a: np.asarray(a, dtype=np.float32)
    w_in = f(w_in)[0]
    m = {}
    m["w1g"] = _lay_fm(f(ffn1_w_gate)[0])
    m["w1u"] = _lay_fm(f(ffn1_w_up)[0])
    m["w1d"] = _lay_fm(f(ffn1_w_down)[0])
    m["w2g"] = _lay_fm(f(ffn2_w_gate)[0])
    m["w2u"] = _lay_fm(f(ffn2_w_up)[0])
    m["w2d"] = _lay_fm(f(ffn2_w_down)[0])
    kr = w_in[:, 1024:1088]
    lat_cols = np.concatenate([w_in[:, 0:1024], kr, kr[:, 32:64], kr[:, 0:32]], axis=1)
    m["wl"] = _lay_fm(lat_cols)
    m["wu"] = _lay_fm(w_in[:, 1088:1088 + 2048])
    wv = w_in[:, 3136:3136 + 2048]
    m["wv"] = np.ascontiguousarray(wv.reshape(2, 8, 128, 4, 512).transpose(3, 0, 2, 1, 4).reshape(8, 128, 8 * 512))
    ga = _lay_fm(w_in[:, 5184:5184 + 2048])
    gc = _lay_fm(w_in[:, 7232:7232 + 2048])
    woa = _lay_fm(f(w_o_attn)[0])
    m["wmix"] = np.ascontiguousarray(np.concatenate([ga, gc, woa], axis=2))
    m["wog"] = _lay_fm(f(w_o_gmlp)[0])
    m["wo"] = _lay_fm(f(w_out)[0])
    wuq, wuk, wuv = f(w_uq)[0], f(w_uk)[0], f(w_uv)[0]
    wh = np.empty((H, 512, 512), np.float32)
    for h in range(H):
        b = h * 192
        wh[h, :, 0:128] = wuq[:, b:b + 128]
        wh[h, :, 128:192] = wuq[:, b + 128:b + 192]
        wh[h, :, 192:224] = wuq[:, b + 160:b + 192]
        wh[h, :, 224:256] = wuq[:, b + 128:b + 160]
        wh[h, :, 256:384] = wuk[:, h * 128:(h + 1) * 128]
        wh[h, :, 384:512] = wuv[:, h * 128:(h + 1) * 128]
    m["wh"] = np.ascontiguousarray(wh.reshape(H, 4, 128, 512).transpose(0, 2, 1, 3).reshape(H, 128, 4 * 512))
    ws_ = f(w_s)[0]
    m["wst"] = np.ascontiguousarray(ws_.transpose(2, 0, 1).reshape(128, 2048))
    m["bsb"] = np.ascontiguousarray(np.broadcast_to(f(b_s)[0].reshape(1, 2048), (128, 2048)))
    vec = np.zeros((128, NVEC), np.float32)

    def put(c0, v):
        v = f(v).reshape(-1)
        n = v.shape[0] // 128
        vec[:, c0:c0 + n] = v.reshape(n, 128).T
    put(V_F1N, ffn1_norm)
    put(V_MIXN, mix_norm)
    put(V_F2N, ffn2_norm)
    put(V_FIN, final_norm)
    put(V_GQ, q_norm)
    put(V_GKV, kv_norm)
    put(V_GV, gmlp_norm)
    put(V_BGA, f(b_gate).reshape(-1)[0:2048])
    put(V_BGC, f(b_gate).reshape(-1)[2048:4096])
    m["vec"] = vec
    m["ident"] = np.eye(128, dtype=np.float32)
    return m


def _rope_table(core):
    inv = 1.0 / (10000.0 ** (np.arange(0, 64, 2, dtype=np.float32) / np.float32(64)))
    pos = np.arange(core * SL, (core + 1) * SL, dtype=np.float32)
    ang = (pos[:, None] * inv[None, :]).astype(np.float32)
    cos = np.cos(ang).astype(np.float32).T
    sin = np.sin(ang).astype(np.float32).T
    tab = np.empty((64, 2 * SL), np.float32)
    tab[0:32, 0:SL] = cos
    tab[32:64, 0:SL] = cos
    tab[0:32, SL:] = -sin
    tab[32:64, SL:] = sin
    return tab


def kernel(x_prompt, x_sample, **weights):
    global _PROGRAM
    xp = np.asarray(x_prompt, dtype=np.float32)
    xs = np.asarray(x_sample, dtype=np.float32)
    xall = np.concatenate([xp, xs], axis=0)
    wm = _prep_weights(**weights)
    if _PROGRAM is None:
        _PROGRAM = build_program()
    nc = _PROGRAM
    in_maps = []
    for c in range(NCORES):
        m = dict(wm)
        m["x"] = np.ascontiguousarray(xall[:, c * SL:(c + 1) * SL, :].reshape(NSEQ * SL, D))
        m["rope"] = _rope_table(c)
        in_maps.append(m)
    res = run_bass_kernel_spmd(nc, in_maps, core_ids=list(range(NCORES)))
    yall = np.empty((NSEQ, SEQ, D), np.float32)
    for c in range(NCORES):
        yall[:, c * SL:(c + 1) * SL, :] = np.asarray(res.results[c]["y"]).reshape(NSEQ, SL, D)
    return (np.ascontiguousarray(yall[0:2]), np.ascontiguousarray(yall[2:3]))
```
